# Optimizing a Trainium2 kernel written in Bass

```python
import math
import jax, jax.numpy as jnp
from jax import lax
import numpy as np

D_MODEL = 2048
BATCH = 4
SEQ = 2048
DEPTH = 2

CTX_LEN = 256
GRID_W = 64
N_MIXERS = 2
N_ATTN_LAYERS = (DEPTH + N_MIXERS - 1) // N_MIXERS
N_HYENA_LAYERS = DEPTH // N_MIXERS

HEAD_DIM = 128
N_HEADS = D_MODEL // HEAD_DIM
N_KV_HEADS = N_HEADS // 4
GQA_GROUP = N_HEADS // N_KV_HEADS
ATTN_WIDTH = N_HEADS * HEAD_DIM
KV_WIDTH = N_KV_HEADS * HEAD_DIM
ATTN_IN = 2 * ATTN_WIDTH + 2 * KV_WIDTH
WINDOW = 128
BLOCK = 128
ROPE_BASE = 10000.0

HYENA_WIDTH = D_MODEL
HYENA_ORDER = 2
HYENA_IN = (HYENA_ORDER + 2) * HYENA_WIDTH
SHORT_CONV = 3
FILTER_EMB = 33
FILTER_HIDDEN = 64
DECAY_FAST = 0.3
DECAY_SLOW = 1.5
DECAY_TARGET = 1e-2
WINDOW_SHIFT = 0.05

NORM_EPS = 1e-6
NEG_INF = -1e30

kernel_name = "hybrid_swa_hyena_prefix_dit"


def rms_norm(x, g):
    x32 = x.astype(jnp.float32)
    y = x32 * lax.rsqrt(jnp.mean(x32 * x32, axis=-1, keepdims=True) + NORM_EPS)
    return (y * g.astype(jnp.float32)).astype(x.dtype)


def modulate(h, shift, scale):
    return h * (1.0 + scale) + shift


def axial_rope_tables(S):
    rows = S // GRID_W
    row = jnp.repeat(jnp.arange(rows), GRID_W).astype(jnp.float32)
    col = jnp.tile(jnp.arange(GRID_W), rows).astype(jnp.float32)
    half = HEAD_DIM // 2
    inv = ROPE_BASE ** (-jnp.arange(0, half, 2, dtype=jnp.float32) / half)
    ang = jnp.stack([row[:, None] * inv[None], col[:, None] * inv[None]], axis=1)
    return jnp.cos(ang), jnp.sin(ang)


def apply_axial_rope(t, cos, sin):
    B, S, nh, _ = t.shape
    half = HEAD_DIM // 2
    q = half // 2
    seg = t.astype(jnp.float32).reshape(B, S, nh, 2, half)
    a, b = seg[..., :q], seg[..., q:]
    cs = cos[None, :, None]
    sn = sin[None, :, None]
    out = jnp.concatenate([a * cs - b * sn, b * cs + a * sn], axis=-1)
    return out.reshape(t.shape).astype(t.dtype)


def _band(t, nb):
    B = t.shape[0]
    tb = t.reshape(B, nb, BLOCK, t.shape[2], t.shape[3])
    tp = jnp.pad(tb, ((0, 0), (1, 1), (0, 0), (0, 0), (0, 0)))
    return jnp.concatenate([tp[:, :-2], tp[:, 1:-1], tp[:, 2:]], axis=2)


def banded_attention(q, k, v, kc, vc, sink):
    B, S = q.shape[0], q.shape[1]
    C = kc.shape[1]
    nb = S // BLOCK
    scale = HEAD_DIM ** -0.5
    qb = q.reshape(B, nb, BLOCK, N_KV_HEADS, GQA_GROUP, HEAD_DIM)
    kb = _band(k, nb)
    vb = _band(v, nb)
    s_loc = jnp.einsum("bnqkgd,bnskd->bnkgqs", qb, kb).astype(jnp.float32) * scale
    qpos = jnp.arange(nb)[:, None] * BLOCK + jnp.arange(BLOCK)[None]
    kpos = jnp.arange(nb)[:, None] * BLOCK - BLOCK + jnp.arange(3 * BLOCK)[None]
    rel = kpos[:, None, :] - qpos[:, :, None]
    valid = (jnp.abs(rel) <= WINDOW) & (kpos[:, None, :] >= 0) & (kpos[:, None, :] < S)
    s_loc = jnp.where(valid[None, :, None, None], s_loc, NEG_INF)
    s_ctx = jnp.einsum("bnqkgd,bckd->bnkgqc", qb, kc).astype(jnp.float32) * scale
    sk = sink.astype(jnp.float32).reshape(N_KV_HEADS, GQA_GROUP)[None, None, :, :, None, None]
    s_sink = jnp.broadcast_to(sk, s_loc.shape[:-1] + (1,))
    p = jax.nn.softmax(jnp.concatenate([s_loc, s_ctx, s_sink], axis=-1), axis=-1).astype(v.dtype)
    nl = 3 * BLOCK
    o = (jnp.einsum("bnkgqs,bnskd->bnqkgd", p[..., :nl], vb)
         + jnp.einsum("bnkgqc,bckd->bnqkgd", p[..., nl:nl + C], vc))
    return o.reshape(B, S, ATTN_WIDTH)


def context_attention(qc, kc, vc, sink):
    B, C = qc.shape[0], qc.shape[1]
    scale = HEAD_DIM ** -0.5
    qg = qc.reshape(B, C, N_KV_HEADS, GQA_GROUP, HEAD_DIM)
    s = jnp.einsum("bqkgd,bskd->bkgqs", qg, kc).astype(jnp.float32) * scale
    sk = sink.astype(jnp.float32).reshape(N_KV_HEADS, GQA_GROUP)[None, :, :, None, None]
    s_sink = jnp.broadcast_to(sk, s.shape[:-1] + (1,))
    p = jax.nn.softmax(jnp.concatenate([s, s_sink], axis=-1), axis=-1).astype(vc.dtype)
    o = jnp.einsum("bkgqs,bskd->bqkgd", p[..., :C], vc)
    return o.reshape(B, C, ATTN_WIDTH)


def attention_layer(hx, hc, w_in, w_out, sink, cos, sin, ctx_out):
    B, S, _ = hx.shape
    C = hc.shape[1]
    px = hx @ w_in
    q = px[..., :ATTN_WIDTH].reshape(B, S, N_HEADS, HEAD_DIM)
    k = px[..., ATTN_WIDTH:ATTN_WIDTH + KV_WIDTH].reshape(B, S, N_KV_HEADS, HEAD_DIM)
    v = px[..., ATTN_WIDTH + KV_WIDTH:ATTN_WIDTH + 2 * KV_WIDTH].reshape(B, S, N_KV_HEADS, HEAD_DIM)
    g = px[..., ATTN_WIDTH + 2 * KV_WIDTH:]
    q = apply_axial_rope(q, cos, sin)
    k = apply_axial_rope(k, cos, sin)
    if ctx_out:
        pc = hc @ w_in
        kvc = pc[..., ATTN_WIDTH:ATTN_WIDTH + 2 * KV_WIDTH]
    else:
        kvc = hc @ w_in[:, ATTN_WIDTH:ATTN_WIDTH + 2 * KV_WIDTH]
    kc = kvc[..., :KV_WIDTH].reshape(B, C, N_KV_HEADS, HEAD_DIM)
    vc = kvc[..., KV_WIDTH:].reshape(B, C, N_KV_HEADS, HEAD_DIM)
    o = banded_attention(q, k, v, kc, vc, sink)
    out_x = (o * jax.nn.silu(g)) @ w_out
    out_c = None
    if ctx_out:
        qc = pc[..., :ATTN_WIDTH].reshape(B, C, N_HEADS, HEAD_DIM)
        gc = pc[..., ATTN_WIDTH + 2 * KV_WIDTH:]
        oc = context_attention(qc, kc, vc, sink)
        out_c = (oc * jax.nn.silu(gc)) @ w_out
    return out_x, out_c


def short_conv(u, w, b):
    L = u.shape[1]
    pad = SHORT_CONV // 2
    up = jnp.pad(u, ((0, 0), (pad, pad), (0, 0)))
    y = up[:, 0:L] * w[0]
    for j in range(1, SHORT_CONV):
        y = y + up[:, j:j + L] * w[j]
    return y + b


def implicit_filters(L, w1, b1, w2, b2, w3, b3, freq):
    f32 = jnp.float32
    t = jnp.linspace(0.0, 1.0, L, dtype=f32)[:, None]
    bands = (FILTER_EMB - 1) // 2
    w = 2.0 * math.pi * jnp.arange(L, dtype=f32) / L
    fr = jnp.linspace(1e-4, bands - 1, bands, dtype=f32)
    ang = w[:, None] * fr[None]
    z = jnp.concatenate([t, jnp.cos(ang), -jnp.sin(ang)], axis=-1)
    fq = freq.astype(f32)
    hid = jnp.sin(fq * (z @ w1.astype(f32) + b1.astype(f32)))
    hid = jnp.sin(fq * (hid @ w2.astype(f32) + b2.astype(f32)))
    hf = (hid @ w3.astype(f32) + b3.astype(f32)).reshape(L, HYENA_ORDER, 2, HYENA_WIDTH)
    max_decay = math.log(DECAY_TARGET) / DECAY_FAST
    min_decay = math.log(DECAY_TARGET) / DECAY_SLOW
    deltas = jnp.linspace(min_decay, max_decay, HYENA_WIDTH, dtype=f32)
    window = jnp.exp(-t * jnp.abs(deltas)[None])[:, None, None, :] + WINDOW_SHIFT
    return hf * window


def bidir_long_conv(u, h_fwd, h_bwd, d):
    L = u.shape[1]
    k = jnp.concatenate([h_fwd, jnp.zeros((1, h_fwd.shape[1]), jnp.float32), h_bwd[1:][::-1]], axis=0)
    u32 = u.astype(jnp.float32)
    U = jnp.fft.rfft(u32, n=2 * L, axis=1)
    K = jnp.fft.rfft(k, n=2 * L, axis=0)
    y = jnp.fft.irfft(U * K[None], n=2 * L, axis=1)[:, :L]
    return (y + u32 * d.astype(jnp.float32)).astype(u.dtype)


def hyena_branch(h, w_in, conv_w, conv_b, w1, b1, w2, b2, w3, b3, freq, bias_d, w_out):
    L = h.shape[1]
    p = h @ w_in
    u = short_conv(p[..., :3 * HYENA_WIDTH], conv_w, conv_b)
    g = p[..., 3 * HYENA_WIDTH:]
    x1 = u[..., :HYENA_WIDTH]
    x2 = u[..., HYENA_WIDTH:2 * HYENA_WIDTH]
    v = u[..., 2 * HYENA_WIDTH:]
    filt = implicit_filters(L, w1, b1, w2, b2, w3, b3, freq)
    z = x1 * bidir_long_conv(v, filt[:, 0, 0], filt[:, 0, 1], bias_d[0])
    y = x2 * bidir_long_conv(z, filt[:, 1, 0], filt[:, 1, 1], bias_d[1])
    return (y * jax.nn.silu(g)) @ w_out


def setup_inputs(seed: int = 0) -> dict:
    key = jax.random.key(seed)
    ks = jax.random.split(key, 24)
    D = D_MODEL
    nrm = jax.random.normal
    return {
        "x": nrm(ks[0], (BATCH, SEQ, D), jnp.float32),
        "c": nrm(ks[1], (BATCH, D), jnp.float32),
        "ctx": nrm(ks[2], (BATCH, CTX_LEN, D), jnp.float32),
        "c_ctx": nrm(ks[3], (D,), jnp.float32),
        "norm_g": 1.0 + 0.02 * nrm(ks[4], (DEPTH, D), jnp.float32),
        "ada_w": 0.5 * D ** -0.5 * nrm(ks[5], (DEPTH, D, 3 * D), jnp.float32),
        "ada_b": 0.02 * nrm(ks[6], (DEPTH, 3 * D), jnp.float32),
        "attn_w_in": D ** -0.5 * nrm(ks[7], (N_ATTN_LAYERS, D, ATTN_IN), jnp.float32),
        "attn_w_out": ATTN_WIDTH ** -0.5 * nrm(ks[8], (N_ATTN_LAYERS, ATTN_WIDTH, D), jnp.float32),
        "attn_sink": 0.5 * nrm(ks[9], (N_ATTN_LAYERS, N_HEADS), jnp.float32),
        "hy_w_in": D ** -0.5 * nrm(ks[10], (N_HYENA_LAYERS, D, HYENA_IN), jnp.float32),
        "hy_conv_w": SHORT_CONV ** -0.5 * nrm(ks[11], (N_HYENA_LAYERS, SHORT_CONV, 3 * HYENA_WIDTH), jnp.float32),
        "hy_conv_b": 0.02 * nrm(ks[12], (N_HYENA_LAYERS, 3 * HYENA_WIDTH), jnp.float32),
        "hy_w1": FILTER_EMB ** -0.5 * nrm(ks[13], (N_HYENA_LAYERS, FILTER_EMB, FILTER_HIDDEN), jnp.float32),
        "hy_b1": 0.02 * nrm(ks[14], (N_HYENA_LAYERS, FILTER_HIDDEN), jnp.float32),
        "hy_w2": FILTER_HIDDEN ** -0.5 * nrm(ks[15], (N_HYENA_LAYERS, FILTER_HIDDEN, FILTER_HIDDEN), jnp.float32),
        "hy_b2": 0.02 * nrm(ks[16], (N_HYENA_LAYERS, FILTER_HIDDEN), jnp.float32),
        "hy_w3": 0.05 * FILTER_HIDDEN ** -0.5 * nrm(ks[17], (N_HYENA_LAYERS, FILTER_HIDDEN, HYENA_ORDER * 2 * HYENA_WIDTH), jnp.float32),
        "hy_b3": 0.01 * nrm(ks[18], (N_HYENA_LAYERS, HYENA_ORDER * 2 * HYENA_WIDTH), jnp.float32),
        "hy_freq": 1.0 + 0.02 * nrm(ks[19], (N_HYENA_LAYERS, FILTER_HIDDEN), jnp.float32),
        "hy_bias_d": 0.1 * nrm(ks[20], (N_HYENA_LAYERS, HYENA_ORDER, HYENA_WIDTH), jnp.float32),
        "hy_w_out": HYENA_WIDTH ** -0.5 * nrm(ks[21], (N_HYENA_LAYERS, HYENA_WIDTH, D), jnp.float32),
        "final_g": 1.0 + 0.02 * nrm(ks[22], (D,), jnp.float32),
    }


def reference(x, c, ctx, c_ctx, norm_g, ada_w, ada_b, attn_w_in, attn_w_out, attn_sink,
              hy_w_in, hy_conv_w, hy_conv_b, hy_w1, hy_b1, hy_w2, hy_b2, hy_w3, hy_b3,
              hy_freq, hy_bias_d, hy_w_out, final_g):
    S = x.shape[1]
    cos, sin = axial_rope_tables(S)
    xc = ctx
    sc = jax.nn.silu(c)
    scc = jax.nn.silu(c_ctx)
    for i in range(DEPTH):
        li = i // N_MIXERS
        ctx_later = any(j % N_MIXERS == 0 for j in range(i + 1, DEPTH))
        mod = sc @ ada_w[i] + ada_b[i]
        shift, scale, gate = jnp.split(mod, 3, axis=-1)
        hx = modulate(rms_norm(x, norm_g[i]), shift[:, None], scale[:, None])
        if i % N_MIXERS == 0:
            mod_c = scc @ ada_w[i] + ada_b[i]
            shift_c, scale_c, gate_c = jnp.split(mod_c, 3, axis=-1)
            hc = modulate(rms_norm(xc, norm_g[i]), shift_c, scale_c)
            out_x, out_c = attention_layer(hx, hc, attn_w_in[li], attn_w_out[li], attn_sink[li],
                                           cos, sin, ctx_later)
            x = x + gate[:, None] * out_x
            if ctx_later:
                xc = xc + gate_c * out_c
        else:
            hp = (hy_w_in[li], hy_conv_w[li], hy_conv_b[li], hy_w1[li], hy_b1[li], hy_w2[li],
                  hy_b2[li], hy_w3[li], hy_b3[li], hy_freq[li], hy_bias_d[li], hy_w_out[li])
            x = x + gate[:, None] * hyena_branch(hx, *hp)
            if ctx_later:
                mod_c = scc @ ada_w[i] + ada_b[i]
                shift_c, scale_c, gate_c = jnp.split(mod_c, 3, axis=-1)
                hc = modulate(rms_norm(xc, norm_g[i]), shift_c, scale_c)
                xc = xc + gate_c * hyena_branch(hc, *hp)
    return rms_norm(x, final_g)
```

```python
import contextlib
import math
import numpy as np
import ml_dtypes
import concourse.bass as bass
import concourse.mybir as mybir
from concourse.bass_utils import run_bass_kernel_spmd

F32 = mybir.dt.float32
BF16 = mybir.dt.bfloat16
ALU = mybir.AluOpType
AF = mybir.ActivationFunctionType
AX = mybir.AxisListType

N_DMA_SEMS = 24
D = 2048
NCORES = 8


class Sched:
    def __init__(self, nc):
        self.nc = nc
        self.ops = []

    @staticmethod
    def _is_psum(t):
        return isinstance(t, tuple) and t[0] in ('ps', 'ptb')

    def add(self, eng, fn, r=(), w=(), dma=False):
        tw = tuple(w)
        w = tuple(w) + tuple(t for t in r if self._is_psum(t) and t not in w)
        self.ops.append(dict(eng=eng, fn=fn, r=tuple(r), w=w, tw=tw, dma=dma))

    def pe(self, fn, r=(), w=()):
        self.add('pe', fn, r, w)

    def act(self, fn, r=(), w=()):
        self.add('act', fn, r, w)

    def dve(self, fn, r=(), w=()):
        self.add('dve', fn, r, w)

    def pool(self, fn, r=(), w=()):
        self.add('pool', fn, r, w)

    def dma(self, fn, r=(), w=(), q='sp'):
        self.add(q, fn, r, w, dma=True)

    def emit(self, final_tokens=()):
        nc = self.nc
        ops = self.ops
        ops.append(dict(eng='sp', fn=None, r=tuple(final_tokens), w=(), tw=(), dma=False))
        n = len(ops)
        last_w = {}
        readers = {}
        deps = [set() for _ in range(n)]
        for i, op in enumerate(ops):
            for t in op['r']:
                if t in last_w:
                    deps[i].add(last_w[t])
            for t in op['w']:
                if t in last_w:
                    deps[i].add(last_w[t])
                for rd in readers.get(t, ()):
                    if rd != i:
                        deps[i].add(rd)
            for t in op['r']:
                readers.setdefault(t, []).append(i)
            for t in op['w']:
                last_w[t] = i
                readers[t] = []
        needed = [False] * n
        for i, op in enumerate(ops):
            keep = set()
            for p in deps[i]:
                po = ops[p]
                if po['eng'] == op['eng'] and not po['dma'] and not op['dma']:
                    if op['eng'] == 'pe':
                        continue
                    if not (set(po['tw']) & set(op['r'])):
                        continue
                keep.add(p)
            deps[i] = keep
            for p in keep:
                needed[p] = True
        stack = contextlib.ExitStack()
        sems = {}
        for e in ('pe', 'act', 'dve', 'pool', 'sp'):
            sems[e] = stack.enter_context(nc.semaphore('s_' + e))
        dsem = {}
        for q in ('sp', 'act', 'pool'):
            dsem[q] = [stack.enter_context(nc.semaphore('d_%s_%d' % (q, k))) for k in range(N_DMA_SEMS)]
        cnt = {e: 0 for e in sems}
        dcnt = {q: [0] * N_DMA_SEMS for q in dsem}
        dnum = {q: 0 for q in dsem}
        inc = [None] * n
        pre = [None] * n
        for i, op in enumerate(ops):
            if op['dma']:
                q = op['eng']
                k = dnum[q] % N_DMA_SEMS
                dnum[q] += 1
                if dcnt[q][k] > 0:
                    pre[i] = (dsem[q][k], dcnt[q][k])
                dcnt[q][k] += 16
                inc[i] = (dsem[q][k], 16, dcnt[q][k])
            elif needed[i]:
                e = op['eng']
                cnt[e] += 1
                inc[i] = (sems[e], 1, cnt[e])
        streams = {e: [] for e in ('pe', 'act', 'dve', 'pool', 'sp')}
        known = {e: {} for e in streams}
        for i, op in enumerate(ops):
            e = op['eng']
            waits = {}
            if pre[i] is not None:
                waits[id(pre[i][0])] = (pre[i][0], pre[i][1])
            for p in deps[i]:
                s, _, v = inc[p]
                if id(s) not in waits or waits[id(s)][1] < v:
                    waits[id(s)] = (s, v)
            wl = []
            for sid, (s, v) in waits.items():
                if known[e].get(sid, 0) >= v:
                    continue
                known[e][sid] = v
                wl.append((s, v))
            streams[e].append((op, wl, inc[i]))
        self.n_ops = n

        def run(eng_obj, st):
            for op, wl, ic in st:
                for s, v in wl:
                    eng_obj.wait_ge(s, v)
                if op['fn'] is None:
                    continue
                ins = op['fn'](eng_obj)
                if ic is not None:
                    ins.then_inc(ic[0], ic[1])

        with stack:
            with nc.Block() as block:
                @block.tensor
                def _(e):
                    run(e, streams['pe'])

                @block.scalar
                def _(e):
                    run(e, streams['act'])

                @block.vector
                def _(e):
                    run(e, streams['dve'])

                @block.gpsimd
                def _(e):
                    run(e, streams['pool'])

                @block.sync
                def _(e):
                    run(e, streams['sp'])


class Ctx:
    def __init__(self):
        self.nc = bass.Bass("TRN2", target_bir_lowering=False)
        self.st = contextlib.ExitStack()
        self.S = Sched(self.nc)

    def din(self, name, shape, dt=F32):
        return self.nc.dram_tensor(name, list(shape), dt, kind="ExternalInput").ap()

    def dout(self, name, shape, dt=F32):
        return self.nc.dram_tensor(name, list(shape), dt, kind="ExternalOutput").ap()

    def sb(self, name, shape, dt):
        return self.st.enter_context(self.nc.sbuf_tensor(name, list(shape), dt))

    def ps(self, name, shape, dt):
        return self.st.enter_context(self.nc.psum_tensor(name, list(shape), dt))


def _rstd_ops(S, ss_ap, rstd_ap, tok_r, tok_w):
    S.dve(lambda e: e.tensor_scalar(out=rstd_ap, in0=ss_ap, scalar1=1.0 / D, scalar2=1e-6,
                                    op0=ALU.mult, op1=ALU.add), r=[tok_r], w=[tok_w])
    S.act(lambda e: e.activation(out=rstd_ap, in_=rstd_ap, func=AF.Sqrt), r=[tok_w], w=[tok_w])
    S.dve(lambda e: e.reciprocal(out=rstd_ap, in_=rstd_ap), r=[tok_w], w=[tok_w])


def build_l0():
    C = Ctx()
    nc, S = C.nc, C.S
    c5T = C.din("c5T", [128, 16, 5])
    adaw = C.din("adaw", [2, 2048, 768])
    adab = C.din("adab", [2, 768])
    modp = C.dout("modp", [5, 2, 768])
    with C.st:
        c_sb = C.sb("c_sb", [128, 16, 5], F32)
        sc = C.sb("sc", [128, 16, 5], BF16)
        w = C.sb("w", [128, 2, 16, 768], BF16)
        b = C.sb("b", [5, 2, 768], F32)
        o = C.sb("o", [5, 2, 768], F32)
        ps = [C.ps("ps%d" % i, [128, 512], F32) for i in range(4)]
        S.dma(lambda e: e.dma_start(out=c_sb[:], in_=c5T), w=['c_sb'])
        for l in range(2):
            S.dma(lambda e, l=l: e.dma_start(out=w[:, l], in_=adaw[l].rearrange("(c p) n -> p c n", p=128)),
                  w=[('w', l)], q='pool')
        S.dma(lambda e: e.dma_start(out=b[:], in_=bass.AP(adab.tensor, 0, [[0, 5], [768, 2], [1, 768]])), w=['b'])
        S.act(lambda e: e.activation(out=sc[:], in_=c_sb[:], func=AF.Silu), r=['c_sb'], w=['sc'])
        for l in range(2):
            for j, (n0, n1) in enumerate(((0, 512), (512, 768))):
                pt = ps[l * 2 + j]
                tk = 'ps%d' % (l * 2 + j)
                for kc in range(16):
                    S.pe(lambda e, l=l, kc=kc, n0=n0, n1=n1, pt=pt: e.matmul(
                        pt[0:5, 0:n1 - n0], lhsT=sc[:, kc, :], rhs=w[:, l, kc, n0:n1],
                        start=(kc == 0), stop=(kc == 15)), r=['sc', ('w', l)], w=[tk])
                S.dve(lambda e, l=l, n0=n0, n1=n1, pt=pt: e.tensor_tensor(
                    out=o[:, l, n0:n1], in0=pt[0:5, 0:n1 - n0], in1=b[:, l, n0:n1], op=ALU.add),
                    r=[tk, 'b'], w=['o'])
        S.dma(lambda e: e.dma_start(out=modp, in_=o[:]), r=['o'], w=['modp'])
        S.emit(final_tokens=['modp'])
    return nc


NW = 1280
NO = 1024


def build_l1(phase=9, dbg=False):
    C = Ctx()
    nc, S = C.nc, C.S
    xw = C.din("xw", [NW, D])
    ctxb = C.din("ctxb", [256, D])
    modv = C.din("modv", [128, 8, 16])
    gate0 = C.din("gate0", [1, D])
    w_in = C.din("w_in", [D, 5120])
    w_out = C.din("w_out", [D, D])
    sink = C.din("sink", [1, 16])
    rope = C.din("rope", [128, 2, NW])
    masks = C.din("masks", [128, 16, 128])
    consts = C.din("consts", [128, 2, 128])
    x1o = C.dout("x1", [NO, D])
    hx1o = C.dout("hx1T", [D, NO], BF16)
    if dbg:
        d_hxT = C.dout("d_hxT", [128, 16, NW], BF16)
        d_kT = C.dout("d_kT", [128, 4, 1536], BF16)
        d_v = C.dout("d_v", [128, 12, 512], BF16)
        d_sg = C.dout("d_sg", [128, 16, 1024], BF16)
    with C.st:
        arena = C.sb("arena", [128, 32768], BF16)
        hxT = arena[:, 0:20480].rearrange("p (c t) -> p c t", c=16)
        hcT = arena[:, 20480:24576].rearrange("p (c t) -> p c t", c=16)
        qT = arena[:, 24576:32768].rearrange("p (b g t) -> p b g t", b=2, g=4)
        woT = arena[:, :].rearrange("p (c n) -> p c n", c=16)
        wbuf = C.sb("wbuf", [128, 2, 16, 512], BF16)
        hx1T = wbuf[:].rearrange("p a c n -> p (a c n)").rearrange("p (c t) -> p c t", c=16)
        kT = C.sb("kT", [128, 4, 1536], BF16)
        v = C.sb("v", [128, 12, 512], BF16)
        sgT = C.sb("sgT", [128, 16, 1024], BF16)
        xt = C.sb("xt", [128, 2, D], F32)
        xs = C.sb("xs", [128, D], BF16)
        rope_sb = C.sb("rope_sb", [128, 2, NW], F32)
        gate_bc = rope_sb[:].rearrange("p a t -> p (a t)")[:, 0:D]
        mask_sb = C.sb("mask_sb", [128, 16, 128], BF16)
        cst = C.sb("cst", [128, 2, 128], BF16)
        ident = cst[:, 0, :]
        perm = cst[:, 1, :]
        ones = C.sb("ones", [128, 128], BF16)
        pT = C.sb("pT", [128, 5, 512], BF16)
        recip = C.sb("recip", [128, 512], F32)
        otmp = C.sb("otmp", [128, 512], F32)
        t1 = C.sb("t1", [128, 512], F32)
        t2 = C.sb("t2", [128, 512], F32)
        qbf = C.sb("qbf", [128, 512], BF16)
        mv = C.sb("mv", [128, 8, 16], F32)
        av = C.sb("av", [128, 6, 16], F32)
        ss = C.sb("ss", [128, 24], F32)
        rstd = C.sb("rstd", [128, 24], F32)
        sk = C.sb("sk", [1, 16], F32)
        esk = C.sb("esk", [1, 16], F32)
        eskrow = C.sb("eskrow", [1, 16, 128], BF16)
        psb = [C.ps("psb%d" % i, [128, 512], F32) for i in range(6)]
        ptb = [C.ps("ptb%d" % i, [128, 1024], BF16) for i in range(2)]

        S.dma(lambda e: e.dma_start(out=mv[:], in_=modv), w=['mv'])
        S.dma(lambda e: e.dma_start(out=rope_sb[:], in_=rope), w=['rope'])
        S.dma(lambda e: e.dma_start(out=sk[:], in_=sink), w=['sk'])
        S.dma(lambda e: e.dma_start(out=mask_sb[:], in_=masks), w=['mask'], q='pool')
        S.dma(lambda e: e.dma_start(out=cst[:], in_=consts), w=['cst'], q='pool')
        S.pool(lambda e: e.memset(ones[:], 1.0), w=['ones'])
        S.pool(lambda e: e.memset(ss[:], 0.0), w=[('ss', i) for i in range(24)])
        for (ai, gi, si, hi) in ((0, 0, 1, 2), (2, 0, 3, 4), (4, 5, 6, 7)):
            S.dve(lambda e, ai=ai, gi=gi, si=si: e.scalar_tensor_tensor(
                out=av[:, ai, :], in0=mv[:, si, :], scalar=1.0, in1=mv[:, gi, :], op0=ALU.add, op1=ALU.mult),
                r=['mv'], w=['av'])
            S.dve(lambda e, ai=ai, hi=hi: e.tensor_copy(out=av[:, ai + 1, :], in_=mv[:, hi, :]), r=['mv'], w=['av'])
        S.act(lambda e: e.activation(out=esk[:], in_=sk[:], func=AF.Exp), r=['sk'], w=['esk'])
        S.dve(lambda e: e.tensor_copy(out=eskrow[:], in_=esk[:].unsqueeze(2).to_broadcast([1, 16, 128])),
              r=['esk'], w=['eskrow'])

        def norm_block(blk, src_ap, dstT, t0, ai, xslot):
            xtile = xt[:, xslot, :]
            xtok = ('xt', xslot)
            S.dma(lambda e: e.dma_start(out=xtile, in_=src_ap), w=[xtok])
            S.act(lambda e: e.activation(out=xs[:], in_=xtile, func=AF.Square, accum_out=ss[:, blk:blk + 1]),
                  r=[xtok], w=['xs', ('ss', blk)])
            _rstd_ops(S, ss[:, blk:blk + 1], rstd[:, blk:blk + 1], ('ss', blk), ('rstd', blk))
            S.dve(lambda e: e.tensor_scalar(out=xs[:], in0=xtile, scalar1=rstd[:, blk:blk + 1], scalar2=None,
                                            op0=ALU.mult), r=[xtok, ('rstd', blk)], w=['xs'])
            for h in range(2):
                for c8 in range(8):
                    c = h * 8 + c8
                    S.pe(lambda e, c=c, c8=c8, h=h: e.transpose(ptb[h][:, c8 * 128:(c8 + 1) * 128],
                                                                xs[:, c * 128:(c + 1) * 128], ident),
                         r=['xs', 'cst'], w=[('ptb', h)])
                for c8 in range(8):
                    c = h * 8 + c8
                    if h == 0:
                        S.act(lambda e, c=c, c8=c8, h=h: e.activation(
                            out=dstT[:, c, t0:t0 + 128], in_=ptb[h][:, c8 * 128:(c8 + 1) * 128], func=AF.Identity,
                            scale=av[:, ai, c:c + 1], bias=av[:, ai + 1, c:c + 1]),
                            r=[('ptb', h), 'av'], w=[('hT', blk)])
                    else:
                        S.dve(lambda e, c=c, c8=c8, h=h: e.tensor_scalar(
                            out=dstT[:, c, t0:t0 + 128], in0=ptb[h][:, c8 * 128:(c8 + 1) * 128],
                            scalar1=av[:, ai, c:c + 1], scalar2=av[:, ai + 1, c:c + 1], op0=ALU.mult, op1=ALU.add),
                            r=[('ptb', h), 'av'], w=[('hT', blk)])

        HX = [('hT', blk) for blk in range(10)]
        HC = [('hT', 10 + j) for j in range(2)]
        for blk in range(10):
            norm_block(blk, xw[blk * 128:(blk + 1) * 128, :], hxT, blk * 128, 0, blk % 2)
        for j in range(2):
            norm_block(10 + j, ctxb[j * 128:(j + 1) * 128, :], hcT, j * 128, 2, j % 2)

        if dbg:
            S.dma(lambda e: e.dma_start(out=d_hxT, in_=hxT), r=HX, w=['d_hxT'])
        if phase < 1:
            S.emit(final_tokens=['d_hxT'])
            return nc
        grp_ctr = [0]

        def load_group(col0, src=None):
            slot = grp_ctr[0] % 2
            grp_ctr[0] += 1
            srcw = w_in if src is None else src
            S.dma(lambda e: e.dma_start(out=wbuf[:, slot],
                                        in_=srcw[:, col0:col0 + 512].rearrange("(c p) n -> p c n", p=128)),
                  w=[('wbuf', slot)], q='pool')
            return slot

        ps_rr = [0]

        def next_ps():
            i = ps_rr[0] % 4
            ps_rr[0] += 1
            return i

        import os
        DBGF = os.environ.get('DBGF', '')

        def rope_tile(pi, dst_ap, tk_dst, t0, n):
            p = psb[pi]
            if 'norope' in DBGF:
                S.act(lambda e: e.copy(out=dst_ap, in_=p[:, :n]), r=[('ps', pi)], w=[tk_dst])
                return
            S.act(lambda e: e.copy(out=qbf[:, :n], in_=p[:, :n]), r=[('ps', pi)], w=['qbf'])
            S.pe(lambda e: e.matmul(psb[4][:, :n], lhsT=perm, rhs=qbf[:, :n], start=True, stop=True),
                 r=['qbf', 'cst'], w=[('ps', 4)])
            if 'rA' in DBGF:
                S.act(lambda e: e.copy(out=dst_ap, in_=psb[4][:, :n]), r=[('ps', pi), ('ps', 4)], w=[tk_dst])
                return
            S.dve(lambda e: e.tensor_tensor(out=t1[:, :n], in0=p[:, :n], in1=rope_sb[:, 0, t0:t0 + n], op=ALU.mult),
                  r=[('ps', pi), 'rope'], w=['t1'])
            S.dve(lambda e: e.tensor_tensor(out=t2[:, :n], in0=psb[4][:, :n], in1=rope_sb[:, 1, t0:t0 + n],
                                            op=ALU.mult), r=[('ps', 4), 'rope'], w=['t2'])
            if 'rB' in DBGF:
                S.dve(lambda e: e.tensor_tensor(out=dst_ap, in0=t1[:, :n], in1=t2[:, :n], op=ALU.add),
                      r=['t1', 't2'], w=[tk_dst])
                return
            S.pool(lambda e: e.tensor_tensor(out=dst_ap, in0=t1[:, :n], in1=t2[:, :n], op=ALU.add),
                   r=['t1', 't2'], w=[tk_dst])

        slot = load_group(2048)
        for h in range(4):
            for (t0, n) in ((0, 512), (512, 512), (1024, 256)):
                pi = next_ps()
                for kc in range(16):
                    S.pe(lambda e, h=h, kc=kc, t0=t0, n=n, pi=pi, slot=slot: e.matmul(
                        psb[pi][:, :n], lhsT=wbuf[:, slot, kc, h * 128:(h + 1) * 128], rhs=hxT[:, kc, t0:t0 + n],
                        start=(kc == 0), stop=(kc == 15)), r=[('wbuf', slot)] + HX, w=[('ps', pi)])
                rope_tile(pi, kT[:, h, t0:t0 + n], ('kT', h), t0, n)
            pi = next_ps()
            for kc in range(16):
                S.pe(lambda e, h=h, kc=kc, pi=pi, slot=slot: e.matmul(
                    psb[pi][:, :256], lhsT=wbuf[:, slot, kc, h * 128:(h + 1) * 128], rhs=hcT[:, kc, :],
                    start=(kc == 0), stop=(kc == 15)), r=[('wbuf', slot)] + HC, w=[('ps', pi)])
            S.act(lambda e, h=h, pi=pi: e.copy(out=kT[:, h, 1280:1536], in_=psb[pi][:, :256]),
                  r=[('ps', pi)], w=[('kT', h)])
        slot = load_group(2560)
        for blk in range(0 if 'nov' in DBGF else 12):
            pi = next_ps()
            for kc in range(16):
                if blk < 10:
                    lh = hxT[:, kc, blk * 128:(blk + 1) * 128]
                    rt = [HX[blk]]
                else:
                    lh = hcT[:, kc, (blk - 10) * 128:(blk - 9) * 128]
                    rt = [HC[blk - 10]]
                S.pe(lambda e, kc=kc, pi=pi, slot=slot, lh=lh: e.matmul(
                    psb[pi][:, :], lhsT=lh, rhs=wbuf[:, slot, kc, :], start=(kc == 0), stop=(kc == 15)),
                    r=[('wbuf', slot)] + rt, w=[('ps', pi)])
            if blk % 2 == 0:
                S.act(lambda e, blk=blk, pi=pi: e.copy(out=v[:, blk, :], in_=psb[pi][:, :]), r=[('ps', pi)], w=['v'])
            else:
                S.dve(lambda e, blk=blk, pi=pi: e.tensor_copy(out=v[:, blk, :], in_=psb[pi][:, :]),
                      r=[('ps', pi)], w=['v'])
        if dbg:
            S.dma(lambda e: e.dma_start(out=d_kT, in_=kT[:]), r=[('kT', h) for h in range(4)], w=['d_kT'])
            S.dma(lambda e: e.dma_start(out=d_v, in_=v[:]), r=['v'], w=['d_v'])
        for gi in range(4 if phase >= 2 else 0):
            slot = load_group(3072 + gi * 512)
            for cc in range(4):
                c = gi * 4 + cc
                for th in range(2):
                    pi = next_ps()
                    for kc in range(16):
                        S.pe(lambda e, cc=cc, kc=kc, th=th, pi=pi, slot=slot: e.matmul(
                            psb[pi][:, :], lhsT=wbuf[:, slot, kc, cc * 128:(cc + 1) * 128],
                            rhs=hxT[:, kc, 128 + th * 512:128 + (th + 1) * 512], start=(kc == 0), stop=(kc == 15)),
                            r=[('wbuf', slot)] + HX, w=[('ps', pi)])
                    S.act(lambda e, c=c, th=th, pi=pi: e.activation(
                        out=sgT[:, c, th * 512:(th + 1) * 512], in_=psb[pi][:, :], func=AF.Silu),
                        r=[('ps', pi)], w=[('sgT', c)])
        for kh in range(4 if phase >= 3 else 0):
            slot = load_group(kh * 512)
            qs = kh % 2
            for g in range(4):
                for th in range(2):
                    pi = next_ps()
                    for kc in range(16):
                        S.pe(lambda e, g=g, kc=kc, th=th, pi=pi, slot=slot: e.matmul(
                            psb[pi][:, :], lhsT=wbuf[:, slot, kc, g * 128:(g + 1) * 128],
                            rhs=hxT[:, kc, 128 + th * 512:128 + (th + 1) * 512], start=(kc == 0), stop=(kc == 15)),
                            r=[('wbuf', slot)] + HX, w=[('ps', pi)])
                    rope_tile(pi, qT[:, qs, g, th * 512:(th + 1) * 512], ('qT', qs), 128 + th * 512, 512)
            for n in range(8):
                chunks = [(n * 128, 2 * n), ((n + 1) * 128, None), ((n + 2) * 128, 2 * n + 1),
                          (1280, None), (1408, None)]
                for j, (k0, mi) in enumerate(chunks):
                    pi = 0 + (j % 2)
                    S.pe(lambda e, k0=k0, pi=pi, n=n, qs=qs, kh=kh: e.matmul(
                        psb[pi][:, :].rearrange("p (g q) -> p g q", g=4), lhsT=kT[:, kh, k0:k0 + 128],
                        rhs=qT[:, qs, :, n * 128:(n + 1) * 128], start=True, stop=True),
                        r=[('kT', kh), ('qT', qs)], w=[('ps', pi)])
                    S.act(lambda e, j=j, pi=pi: e.activation(out=pT[:, j, :], in_=psb[pi][:, :], func=AF.Exp,
                                                             scale=128.0 ** -0.5), r=[('ps', pi)], w=[('pT', j)])
                    if mi is not None:
                        S.pool(lambda e, j=j, mi=mi: e.tensor_tensor(
                            out=pT[:, j, :].rearrange("p (g q) -> p g q", g=4),
                            in0=pT[:, j, :].rearrange("p (g q) -> p g q", g=4),
                            in1=mask_sb[:, mi, :].unsqueeze(1).to_broadcast([128, 4, 128]), op=ALU.mult),
                            r=[('pT', j), 'mask'], w=[('pT', j)])
                for j, (k0, mi) in enumerate(chunks):
                    S.pe(lambda e, j=j: e.matmul(psb[2][:, :], lhsT=ones[:], rhs=pT[:, j, :], start=(j == 0), stop=False),
                         r=[('pT', j), 'ones'], w=[('ps', 2)])
                S.pe(lambda e, kh=kh: e.matmul(psb[2][:, :].rearrange("p (g q) -> p g q", g=4), lhsT=ones[0:1, :],
                                               rhs=eskrow[0:1, kh * 4:(kh + 1) * 4, :], start=False, stop=True),
                     r=['eskrow', 'ones'], w=[('ps', 2)])
                for j, (k0, mi) in enumerate(chunks):
                    vb = (k0 // 128) if k0 < 1280 else (10 + (k0 - 1280) // 128)
                    S.pe(lambda e, j=j, vb=vb, kh=kh: e.matmul(
                        psb[3][:, :], lhsT=v[:, vb, kh * 128:(kh + 1) * 128], rhs=pT[:, j, :],
                        start=(j == 0), stop=(j == 4)), r=[('pT', j), 'v'], w=[('ps', 3)])
                S.dve(lambda e: e.reciprocal(out=recip[:], in_=psb[2][:, :]), r=[('ps', 2)], w=['recip'])
                S.dve(lambda e: e.tensor_tensor(out=otmp[:], in0=psb[3][:, :], in1=recip[:], op=ALU.mult),
                      r=[('ps', 3), 'recip'], w=['otmp'])
                sgv = sgT[:, kh * 4:(kh + 1) * 4, n * 128:(n + 1) * 128]
                S.pool(lambda e, sgv=sgv: e.tensor_tensor(
                    out=sgv, in0=otmp[:].rearrange("p (g q) -> p g q", g=4), in1=sgv, op=ALU.mult),
                    r=['otmp'] + [('sgT', kh * 4 + g) for g in range(4)], w=[('sgT', kh * 4 + g) for g in range(4)])

        if dbg:
            S.dma(lambda e: e.dma_start(out=d_sg, in_=sgT[:]), r=[('sgT', c) for c in range(16)], w=['d_sg'])
        if phase < 4:
            S.emit(final_tokens=['d_hxT', 'd_kT', 'd_v', 'd_sg'])
            return nc
        ARENA = HX + HC + [('qT', 0), ('qT', 1)]
        for cg in range(4):
            S.dma(lambda e, cg=cg: e.dma_start(
                out=woT[:, :, cg * 512:(cg + 1) * 512],
                in_=w_out[:, cg * 512:(cg + 1) * 512].rearrange("(c p) n -> p c n", p=128)),
                r=([] if cg == 0 else ['arena_free']), w=((ARENA + ['arena_free']) if cg == 0 else []) + [('wo', cg)], q='pool')
        S.dma(lambda e: e.dma_start(out=gate_bc, in_=bass.AP(gate0.tensor, 0, [[0, 128], [1, D]])),
              w=['rope', 'gate'])
        SG = [('sgT', c) for c in range(16)]
        for n in range(8):
            xtile = xt[:, 0, :]
            x1tile = xt[:, 1, :]
            S.dma(lambda e, n=n: e.dma_start(out=xtile, in_=xw[128 + n * 128:128 + (n + 1) * 128, :]), w=[('xt', 0)])
            for cg in range(4):
                pi = cg
                for c in range(16):
                    S.pe(lambda e, c=c, cg=cg, pi=pi, n=n: e.matmul(
                        psb[pi][:, :], lhsT=sgT[:, c, n * 128:(n + 1) * 128], rhs=woT[:, c, cg * 512:(cg + 1) * 512],
                        start=(c == 0), stop=(c == 15)), r=SG + [('wo', cg)], w=[('ps', pi)])
                S.dve(lambda e, cg=cg, pi=pi: e.tensor_tensor(
                    out=x1tile[:, cg * 512:(cg + 1) * 512], in0=psb[pi][:, :], in1=gate_bc[:, cg * 512:(cg + 1) * 512],
                    op=ALU.mult), r=[('ps', pi), 'gate'], w=[('xt', 1)])
                S.pool(lambda e, cg=cg: e.tensor_tensor(
                    out=x1tile[:, cg * 512:(cg + 1) * 512], in0=x1tile[:, cg * 512:(cg + 1) * 512],
                    in1=xtile[:, cg * 512:(cg + 1) * 512], op=ALU.add), r=[('xt', 0), ('xt', 1)], w=[('xt', 1)])
            S.dma(lambda e, n=n: e.dma_start(out=x1o[n * 128:(n + 1) * 128, :], in_=x1tile), r=[('xt', 1)], w=['x1o'])
            blk = 12 + n
            S.act(lambda e, blk=blk: e.activation(out=xs[:], in_=x1tile, func=AF.Square, accum_out=ss[:, blk:blk + 1]),
                  r=[('xt', 1)], w=['xs', ('ss', blk)])
            _rstd_ops(S, ss[:, blk:blk + 1], rstd[:, blk:blk + 1], ('ss', blk), ('rstd', blk))
            S.dve(lambda e, blk=blk: e.tensor_scalar(out=xs[:], in0=x1tile, scalar1=rstd[:, blk:blk + 1], scalar2=None,
                                                     op0=ALU.mult), r=[('xt', 1), ('rstd', blk)], w=['xs'])
            for h in range(2):
                for c8 in range(8):
                    c = h * 8 + c8
                    S.pe(lambda e, c=c, c8=c8, h=h: e.transpose(ptb[h][:, c8 * 128:(c8 + 1) * 128],
                                                                xs[:, c * 128:(c + 1) * 128], ident),
                         r=['xs', 'cst'], w=[('ptb', h)])
                for c8 in range(8):
                    c = h * 8 + c8
                    S.act(lambda e, c=c, c8=c8, h=h, n=n: e.activation(
                        out=hx1T[:, c, n * 128:(n + 1) * 128], in_=ptb[h][:, c8 * 128:(c8 + 1) * 128], func=AF.Identity,
                        scale=av[:, 4, c:c + 1], bias=av[:, 5, c:c + 1]),
                        r=[('ptb', h), 'av'], w=[('wbuf', 0), ('wbuf', 1)])
        S.dma(lambda e: e.dma_start(out=hx1o.rearrange("(c p) t -> p c t", p=128), in_=hx1T),
              r=[('wbuf', 0), ('wbuf', 1)], w=['hx1o'])
        S.emit(final_tokens=['x1o', 'hx1o'] + (['d_hxT', 'd_kT', 'd_v', 'd_sg'] if dbg else []))
    return nc


def _pl(vec):
    return np.ascontiguousarray(np.asarray(vec, np.float32).reshape(16, 128).T)


def _rope_tables(pos):
    pos = np.asarray(pos, np.int64)
    row = (pos // 64).astype(np.float64)
    col = (pos % 64).astype(np.float64)
    inv = 10000.0 ** (-np.arange(0, 64, 2, dtype=np.float64) / 64)
    cosT = np.zeros((128, len(pos)), np.float32)
    sinT = np.zeros((128, len(pos)), np.float32)
    for d in range(128):
        axis = d // 64
        r = d % 64
        f = r % 32
        ang = (row if axis == 0 else col) * inv[f]
        cosT[d] = np.cos(ang)
        sinT[d] = (-np.sin(ang)) if r < 32 else np.sin(ang)
    return cosT, sinT


def _consts_l1():
    ident = np.eye(128, dtype=np.float32)
    perm = np.zeros((128, 128), np.float32)
    for m in range(128):
        r = m % 64
        k = m + 32 if r < 32 else m - 32
        perm[k, m] = 1.0
    return np.ascontiguousarray(np.stack([ident, perm], 1))


def _masks_l1(half):
    m = np.zeros((128, 16, 128), np.float32)
    j = np.arange(128)[:, None]
    i = np.arange(128)[None, :]
    for n in range(8):
        gb = half * 8 + n
        if gb > 0:
            m[:, 2 * n, :] = (j >= i)
        if gb < 15:
            m[:, 2 * n + 1, :] = (j <= i)
    return m


def run_l0(inp):
    c5 = np.concatenate([inp["c"], inp["c_ctx"][None]], 0)
    c5T = np.ascontiguousarray(c5.T.reshape(16, 128, 5).transpose(1, 0, 2))
    maps = []
    for i in range(NCORES):
        maps.append({"c5T": c5T,
                     "adaw": np.ascontiguousarray(inp["ada_w"][:, :, i * 768:(i + 1) * 768]),
                     "adab": np.ascontiguousarray(inp["ada_b"][:, i * 768:(i + 1) * 768])})
    res = run_bass_kernel_spmd(build_l0(), maps, core_ids=list(range(NCORES)))
    mod = np.concatenate([r["modp"] for r in res.results], axis=2)
    return np.ascontiguousarray(mod.transpose(1, 0, 2))


def run_l1(inp, mod, phase=9, dbg=False, ncores=NCORES):
    maps = []
    consts = _consts_l1()
    for i in range(ncores):
        b, half = i // 2, i % 2
        lo = half * 1024 - 128
        xw = np.zeros((NW, D), np.float32)
        s0, s1 = max(lo, 0), min(lo + NW, 2048)
        xw[s0 - lo:s1 - lo] = inp["x"][b, s0:s1]
        pos = np.clip(np.arange(lo, lo + NW), 0, 2047)
        cosT, sinT = _rope_tables(pos)
        shift0, scale0, gate0 = np.split(mod[0, b], 3)
        shift0c, scale0c, _ = np.split(mod[0, 4], 3)
        shift1, scale1, _ = np.split(mod[1, b], 3)
        modv = np.stack([_pl(inp["norm_g"][0]), _pl(scale0), _pl(shift0), _pl(scale0c), _pl(shift0c),
                         _pl(inp["norm_g"][1]), _pl(scale1), _pl(shift1)], 1)
        maps.append({"xw": xw, "ctxb": np.ascontiguousarray(inp["ctx"][b]), "modv": np.ascontiguousarray(modv),
                     "gate0": np.ascontiguousarray(gate0[None]), "w_in": np.ascontiguousarray(inp["attn_w_in"][0]),
                     "w_out": np.ascontiguousarray(inp["attn_w_out"][0]),
                     "sink": np.ascontiguousarray(inp["attn_sink"]),
                     "rope": np.ascontiguousarray(np.stack([cosT, sinT], 1)),
                     "masks": _masks_l1(half), "consts": consts})
    res = run_bass_kernel_spmd(build_l1(phase, dbg), maps, core_ids=list(range(ncores)))
    if dbg:
        return res.results
    x1 = np.zeros((4, 2048, D), np.float32)
    hx1T = np.zeros((4, D, 2048), ml_dtypes.bfloat16)
    for i, r in enumerate(res.results):
        b, half = i // 2, i % 2
        x1[b, half * 1024:(half + 1) * 1024] = r["x1"]
        hx1T[b, :, half * 1024:(half + 1) * 1024] = r["hx1T"]
    return x1, hx1T


def build_l2(npairs=2, dbg=False):
    C = Ctx()
    nc, S = C.nc, C.S
    hx = C.din("hx1T", [4, D, 2048], BF16)
    w_in = C.din("w_in", [D, 1024])
    cw = C.din("cw", [128, 6, 4])
    dd = C.din("dd", [128, 2, 2])
    zf = C.din("zf", [33, 2048])
    fw1 = C.din("fw1", [33, 64])
    fw2 = C.din("fw2", [64, 64])
    fw3 = C.din("fw3", [65, 1024])
    fvec = C.din("fvec", [64, 4])
    win = C.din("win", [16, 128, 2, 256])
    Ftab = C.din("Ftab", [8, 128, 16 * 2 * 256], BF16)
    Gtab = C.din("Gtab", [4, 2, 128, 16 * 512], BF16)
    ident_d = C.din("identd", [128, 128])
    ygo = C.dout("yg", [128, 2, 4, 2048], BF16)
    if dbg:
        d_K = C.dout("d_K", [128, 16, 2, 2, 256], BF16)
        d_h = C.dout("d_h", [128, 16, 1024], BF16)
        d_v = C.dout("d_v", [128, 2, 2, 2048], BF16)
        d_z = C.dout("d_z", [128, 2, 2, 2048], BF16)
    with C.st:
        arena = C.sb("arena", [128, 49152], BF16)
        hxs = arena[:, 0:32768].rearrange("p (c t) -> p c t", c=16)
        wis = arena[:, 32768:49152].rearrange("p (c n) -> p c n", c=16)
        tabs = arena[:, 0:16384].rearrange("p (s n) -> p s n", s=2)
        Yh = arena[:, 16384:32768].rearrange("p (f a n) -> p f a n", f=16, a=2)
        hfil = arena[:, 16384:32768].rearrange("p (k n) -> p k n", k=16)
        vT = arena[:, 32768:40960].rearrange("p (k n) -> p k n", k=16)
        tmpf = arena[:, 40960:49152].bitcast(F32).rearrange("p (s n) -> p s n", s=8)
        x1s = C.sb("x1s", [128, 2, 2, 2048], BF16)
        x2s = C.sb("x2s", [128, 2, 2, 2048], BF16)
        vch = C.sb("vch", [128, 2, 2, 2048], BF16)
        pc = C.sb("pc", [128, 2050], F32)
        u1 = C.sb("u1", [128, 2048], F32)
        sgt = C.sb("sgt", [128, 2, 2048], BF16)
        Kh = C.sb("Kh", [128, 16, 2, 2, 256], BF16)
        cws = C.sb("cws", [128, 6, 4], F32)
        dds = C.sb("dds", [128, 2, 2], F32)
        ident = C.sb("ident", [128, 128], BF16)
        zfs = pc[:, 0:2048]
        w1s = C.sb("w1s", [33, 64], F32)
        w2s = C.sb("w2s", [64, 64], F32)
        w3s = u1[0:65, 0:1024]
        wins = u1[:, 1024:2048].rearrange("p (s a c) -> p s a c", s=2, a=2)
        fvs = C.sb("fvs", [64, 4], F32)
        hidv = arena[:, 32768:40960].bitcast(F32)
        hid1 = hidv[:, 0:2048]
        hid2 = hidv[:, 2048:4096]
        psb = [C.ps("psb%d" % i, [128, 512], F32) for i in range(6)]
        ptb = [C.ps("ptb%d" % i, [128, 1024], BF16) for i in range(2)]

        S.dma(lambda e: e.dma_start(out=cws[:], in_=cw), w=['cws'])
        S.dma(lambda e: e.dma_start(out=dds[:], in_=dd), w=['dds'])
        S.dma(lambda e: e.dma_start(out=zfs[0:33, :], in_=zf), w=['zfs'])
        S.dma(lambda e: e.dma_start(out=w1s[:], in_=fw1), w=['w1s'])
        S.dma(lambda e: e.dma_start(out=w2s[:], in_=fw2), w=['w2s'])
        S.dma(lambda e: e.dma_start(out=w3s, in_=fw3), w=['w3s'])
        S.dma(lambda e: e.dma_start(out=fvs[:], in_=fvec), w=['fvs'])
        S.dma(lambda e: e.dma_start(out=ident[:], in_=ident_d), w=['ident'], q='pool')
        S.pool(lambda e: e.memset(hid2[64:65, :], 1.0), w=['hid2'])

        def sin_layer(src, wsb, kdim, bcol, dst, tsrc, tdst):
            for tr in range(4):
                pi = tr
                S.pe(lambda e, tr=tr, pi=pi: e.matmul(psb[pi][0:64, :], lhsT=wsb[0:kdim, :],
                                                      rhs=src[0:kdim, tr * 512:(tr + 1) * 512], start=True, stop=True),
                     r=[tsrc, 'w1s', 'w2s'], w=[('ps', pi)])
                S.dve(lambda e, tr=tr, pi=pi: e.tensor_scalar(
                    out=dst[0:64, tr * 512:(tr + 1) * 512], in0=psb[pi][0:64, :], scalar1=fvs[:, bcol:bcol + 1],
                    scalar2=fvs[:, 2:3], op0=ALU.add, op1=ALU.mult), r=[('ps', pi), 'fvs'], w=[tdst])
                kk = tmpf[0:64, 2 + (tr % 2), :]
                tkk = ('tmp', 2 + (tr % 2))
                S.dve(lambda e, tr=tr, kk=kk: e.tensor_scalar(
                    out=kk, in0=dst[0:64, tr * 512:(tr + 1) * 512], scalar1=1.0 / (2.0 * math.pi), scalar2=12582912.0,
                    op0=ALU.mult, op1=ALU.add), r=[tdst], w=[tkk])
                S.dve(lambda e, kk=kk: e.tensor_scalar(out=kk, in0=kk, scalar1=-12582912.0, scalar2=None, op0=ALU.add),
                      r=[tkk], w=[tkk])
                S.dve(lambda e, tr=tr, kk=kk: e.scalar_tensor_tensor(
                    out=dst[0:64, tr * 512:(tr + 1) * 512], in0=kk, scalar=-2.0 * math.pi,
                    in1=dst[0:64, tr * 512:(tr + 1) * 512], op0=ALU.mult, op1=ALU.add), r=[tkk, tdst], w=[tdst])
                S.act(lambda e, tr=tr: e.activation(
                    out=dst[0:64, tr * 512:(tr + 1) * 512], in_=dst[0:64, tr * 512:(tr + 1) * 512], func=AF.Sin),
                    r=[tdst], w=[tdst])

        sin_layer(zfs, w1s, 33, 0, hid1, 'zfs', 'hid1')
        sin_layer(hid1, w2s, 64, 1, hid2, 'hid1', 'hid2')
        S.pool(lambda e: e.memset(pc[:], 0.0), w=['pc', 'zfs'])
        for kt in range(16):
            ws = kt % 2
            S.dma(lambda e, kt=kt, ws=ws: e.dma_start(out=wins[:, ws], in_=win[kt]), w=[('wins', ws)])
            for half in range(2):
                pi = 4 + half
                S.pe(lambda e, kt=kt, half=half, pi=pi: e.matmul(
                    psb[pi][:, :], lhsT=hid2[0:65, kt * 128:(kt + 1) * 128], rhs=w3s[0:65, half * 512:(half + 1) * 512],
                    start=True, stop=True), r=['hid2', 'w3s'], w=[('ps', pi)])
                S.dve(lambda e, kt=kt, half=half, pi=pi, ws=ws: e.tensor_tensor(
                    out=hfil[:, kt, half * 512:(half + 1) * 512], in0=psb[pi][:, :],
                    in1=wins[:, ws].rearrange("p a c -> p (a c)"), op=ALU.mult),
                    r=[('ps', pi), ('wins', ws)], w=['Yh'])
        if dbg:
            S.dma(lambda e: e.dma_start(out=d_h, in_=hfil), r=['Yh'], w=['d_h'])

        tab_ctr = [0]

        def fwd_dft(src_fn, tsrc, consume):
            for g in range(8):
                slot = tab_ctr[0] % 2
                tab_ctr[0] += 1
                S.dma(lambda e, g=g, slot=slot: e.dma_start(out=tabs[:, slot], in_=Ftab[g]), r=['inproj_done'],
                      w=[('tab', slot)])
                tv = tabs[:, slot].rearrange("p (k a f) -> p k a f", k=16, a=2)
                for j in range(2):
                    ft = g * 2 + j
                    pr, pS = (0, 1) if ft % 2 == 0 else (2, 3)
                    for a, pi in ((0, pr), (1, pS)):
                        for kc in range(16):
                            S.pe(lambda e, kc=kc, a=a, pi=pi, j=j, tv=tv: e.matmul(
                                psb[pi][:, :], lhsT=tv[:, kc, a, j * 128:(j + 1) * 128], rhs=src_fn(kc),
                                start=(kc == 0), stop=(kc == 15)), r=[('tab', slot), tsrc], w=[('ps', pi)])
                    consume(ft, pr, pS)

        for o in range(2):
            def consume_f(ft, pr, pS, o=o):
                S.act(lambda e: e.copy(out=tmpf[:, 0, 0:256], in_=psb[pr][:, 256:512]), r=[('ps', pr)], w=[('tmp', 0)])
                S.dve(lambda e: e.tensor_tensor(out=Kh[:, ft, o, 0, :], in0=psb[pr][:, 0:256], in1=tmpf[:, 0, 0:256],
                                                op=ALU.add), r=[('ps', pr), ('tmp', 0)], w=['Kh'])
                S.act(lambda e: e.copy(out=tmpf[:, 1, 0:256], in_=psb[pS][:, 0:256]), r=[('ps', pS)], w=[('tmp', 1)])
                S.dve(lambda e: e.tensor_tensor(out=Kh[:, ft, o, 1, :], in0=psb[pS][:, 256:512], in1=tmpf[:, 1, 0:256],
                                                op=ALU.subtract), r=[('ps', pS), ('tmp', 1)], w=['Kh'])
            fwd_dft(lambda kc, o=o: hfil[:, kc, o * 512:(o + 1) * 512], 'Yh', consume_f)
        if dbg:
            S.dma(lambda e: e.dma_start(out=d_K, in_=Kh[:]), r=['Kh'], w=['d_K'])

        def consume_data(o):
            def f(ft, pr, pS):
                xr, xs_, ta, tb, tc, td = (tmpf[:, i, :] for i in range(6))
                kre = Kh[:, ft, o, 0, :].unsqueeze(1).to_broadcast([128, 2, 256])
                kim = Kh[:, ft, o, 1, :].unsqueeze(1).to_broadcast([128, 2, 256])
                v3 = lambda ap: ap.rearrange("p (b c) -> p b c", b=2)
                S.act(lambda e: e.copy(out=xr, in_=psb[pr][:, :]), r=[('ps', pr)], w=[('tmp', 0)])
                S.act(lambda e: e.copy(out=xs_, in_=psb[pS][:, :]), r=[('ps', pS)], w=[('tmp', 1)])
                S.pool(lambda e: e.tensor_tensor(out=v3(ta), in0=v3(xr), in1=kre, op=ALU.mult), r=[('tmp', 0), 'Kh'], w=[('tmp', 2)])
                S.dve(lambda e: e.tensor_tensor(out=v3(tb), in0=v3(xs_), in1=kim, op=ALU.mult), r=[('tmp', 1), 'Kh'], w=[('tmp', 3)])
                S.pool(lambda e: e.tensor_tensor(out=Yh[:, ft, 0, :], in0=ta, in1=tb, op=ALU.add),
                       r=[('tmp', 2), ('tmp', 3)], w=['Yh'])
                S.dve(lambda e: e.tensor_tensor(out=v3(tc), in0=v3(xs_), in1=kre, op=ALU.mult), r=[('tmp', 1), 'Kh'], w=[('tmp', 4)])
                S.pool(lambda e: e.tensor_tensor(out=v3(td), in0=v3(xr), in1=kim, op=ALU.mult), r=[('tmp', 0), 'Kh'], w=[('tmp', 5)])
                S.dve(lambda e: e.tensor_tensor(out=Yh[:, ft, 1, :], in0=tc, in1=td, op=ALU.subtract),
                      r=[('tmp', 4), ('tmp', 5)], w=['Yh'])
            return f

        def inv_dft(consume):
            for tr in range(4):
                for half in range(2):
                    slot = tab_ctr[0] % 2
                    tab_ctr[0] += 1
                    S.dma(lambda e, tr=tr, half=half, slot=slot: e.dma_start(out=tabs[:, slot], in_=Gtab[tr, half]),
                          r=['inproj_done'], w=[('tab', slot)])
                    gv = tabs[:, slot].rearrange("p (k t) -> p k t", k=16)
                    for cc in range(4):
                        for kc in range(16):
                            kcg = half * 16 + kc
                            S.pe(lambda e, cc=cc, kc=kc, kcg=kcg, gv=gv: e.matmul(
                                psb[cc][:, :], lhsT=Yh[:, kcg // 2, kcg % 2, cc * 128:(cc + 1) * 128], rhs=gv[:, kc, :],
                                start=(kcg == 0), stop=(kcg == 31)), r=[('tab', slot), 'Yh'], w=[('ps', cc)])
                for cc in range(4):
                    consume(tr, cc)

        def transposes(src4, tsrc):
            for kt in range(16):
                h = kt % 2
                for b in range(2):
                    for cch in range(2):
                        col = b * 256 + cch * 128
                        S.pe(lambda e, kt=kt, b=b, cch=cch, col=col, h=h: e.transpose(
                            ptb[h][:, col:col + 128], src4[:, cch, b, kt * 128:(kt + 1) * 128], ident[:]),
                            r=[tsrc, 'ident'], w=[('ptb', h)])
                if h == 0:
                    S.act(lambda e, kt=kt, h=h: e.copy(out=vT[:, kt, :], in_=ptb[h][:, 0:512]), r=[('ptb', h)], w=['vT'])
                else:
                    S.dve(lambda e, kt=kt, h=h: e.tensor_copy(out=vT[:, kt, :], in_=ptb[h][:, 0:512]), r=[('ptb', h)], w=['vT'])

        for pair in range(npairs):
            S.dma(lambda e: e.dma_start(out=wis, in_=w_in.rearrange("(c p) n -> p c n", p=128)),
                  w=['wis', 'vT', 'hid1', 'hid2'] + [('tmp', i) for i in range(8)], q='pool')
            for bb in range(2):
                b = pair * 2 + bb
                S.dma(lambda e, b=b: e.dma_start(out=hxs, in_=hx[b].rearrange("(c p) t -> p c t", p=128)),
                      w=['hxs', ('tab', 0), ('tab', 1), 'Yh'])
                for (kind, cch) in (('g', 0), ('g', 1), ('x2', 0), ('x2', 1), ('x1', 0), ('x1', 1), ('v', 0), ('v', 1)):
                    col0 = {'x1': 0, 'x2': 256, 'v': 512, 'g': 768}[kind] + cch * 128
                    for tr in range(4):
                        for kc in range(16):
                            S.pe(lambda e, tr=tr, kc=kc, col0=col0: e.matmul(
                                psb[tr][:, :], lhsT=wis[:, kc, col0:col0 + 128], rhs=hxs[:, kc, tr * 512:(tr + 1) * 512],
                                start=(kc == 0), stop=(kc == 15)), r=['wis', 'hxs'],
                                w=[('ps', tr)] + (['inproj_done'] if (kind, cch, tr, kc) == ('v', 1, 3, 15) else []))
                    if kind == 'g':
                        for tr in range(4):
                            S.act(lambda e, tr=tr, cch=cch: e.activation(
                                out=sgt[:, cch, tr * 512:(tr + 1) * 512], in_=psb[tr][:, :], func=AF.Silu),
                                r=[('ps', tr)], w=[('sgt', cch)])
                        continue
                    ci = {'x1': 0, 'x2': 2, 'v': 4}[kind] + cch
                    for tr in range(4):
                        S.act(lambda e, tr=tr: e.copy(out=pc[:, 1 + tr * 512:1 + (tr + 1) * 512], in_=psb[tr][:, :]),
                              r=[('ps', tr)], w=['pc'])
                    S.dve(lambda e, ci=ci: e.tensor_scalar(out=u1[:], in0=pc[:, 1:2049], scalar1=cws[:, ci, 1:2],
                                                           scalar2=cws[:, ci, 3:4], op0=ALU.mult, op1=ALU.add),
                          r=['pc', 'cws'], w=['u1', 'w3s', ('wins', 0), ('wins', 1)])
                    S.dve(lambda e, ci=ci: e.scalar_tensor_tensor(out=u1[:], in0=pc[:, 0:2048], scalar=cws[:, ci, 0:1],
                                                                   in1=u1[:], op0=ALU.mult, op1=ALU.add),
                           r=['pc', 'cws', 'u1'], w=['u1'])
                    if kind == 'x2':
                        S.dve(lambda e, ci=ci: e.scalar_tensor_tensor(out=u1[:], in0=pc[:, 2:2050], scalar=cws[:, ci, 2:3],
                                                                      in1=u1[:], op0=ALU.mult, op1=ALU.add),
                              r=['pc', 'cws', 'u1'], w=['u1'])
                        S.pool(lambda e, cch=cch, bb=bb: e.tensor_tensor(out=x2s[:, cch, bb, :], in0=u1[:], in1=sgt[:, cch, :],
                                                                         op=ALU.mult), r=['u1', ('sgt', cch)], w=['x2s'])
                    else:
                        dst = x1s if kind == 'x1' else vch
                        S.dve(lambda e, ci=ci, dst=dst, cch=cch, bb=bb: e.scalar_tensor_tensor(
                            out=dst[:, cch, bb, :], in0=pc[:, 2:2050], scalar=cws[:, ci, 2:3], in1=u1[:],
                            op0=ALU.mult, op1=ALU.add), r=['pc', 'cws', 'u1'], w=['x1s' if kind == 'x1' else 'vch'])
            if dbg and pair == 0:
                S.dma(lambda e: e.dma_start(out=d_v, in_=vch[:]), r=['vch'], w=['d_v'])
            transposes(vch, 'vch')
            fwd_dft(lambda kc: vT[:, kc, :], 'vT', consume_data(0))

            def consume_c1(tr, cc):
                bb, cch = cc // 2, cc % 2
                tt = tmpf[:, 6 + (cc % 2), :]
                tk = ('tmp', 6 + (cc % 2))
                sl = slice(tr * 512, (tr + 1) * 512)
                S.dve(lambda e: e.scalar_tensor_tensor(out=tt, in0=vch[:, cch, bb, sl], scalar=dds[:, 0, cch:cch + 1],
                                                       in1=psb[cc][:, :], op0=ALU.mult, op1=ALU.add),
                      r=['vch', ('ps', cc), 'dds'], w=[tk])
                S.pool(lambda e: e.tensor_tensor(out=vch[:, cch, bb, sl], in0=tt, in1=x1s[:, cch, bb, sl], op=ALU.mult),
                       r=[tk, 'x1s'], w=['vch'])
            inv_dft(consume_c1)
            if dbg and pair == 0:
                S.dma(lambda e: e.dma_start(out=d_z, in_=vch[:]), r=['vch'], w=['d_z'])
            transposes(vch, 'vch')
            fwd_dft(lambda kc: vT[:, kc, :], 'vT', consume_data(1))

            def consume_c2(tr, cc):
                bb, cch = cc // 2, cc % 2
                tt = tmpf[:, 6 + (cc % 2), :]
                tk = ('tmp', 6 + (cc % 2))
                sl = slice(tr * 512, (tr + 1) * 512)
                S.dve(lambda e: e.scalar_tensor_tensor(out=tt, in0=vch[:, cch, bb, sl], scalar=dds[:, 1, cch:cch + 1],
                                                       in1=psb[cc][:, :], op0=ALU.mult, op1=ALU.add),
                      r=['vch', ('ps', cc), 'dds'], w=[tk])
                S.pool(lambda e: e.tensor_tensor(out=x2s[:, cch, bb, sl], in0=tt, in1=x2s[:, cch, bb, sl], op=ALU.mult),
                       r=[tk, 'x2s'], w=['x2s'])
            inv_dft(consume_c2)
            S.dma(lambda e, pair=pair: e.dma_start(out=ygo[:, :, pair * 2:pair * 2 + 2, :], in_=x2s[:]), r=['x2s'], w=['ygo'])
        fin = ['ygo'] + (['d_K', 'd_h', 'd_v', 'd_z'] if dbg else [])
        S.emit(final_tokens=fin)
    return nc


def build_l3():
    C = Ctx()
    nc, S = C.nc, C.S
    ygT = C.din("ygT", [128, 16, NO], BF16)
    w_out = C.din("w_out", [D, D])
    x1 = C.din("x1", [NO, D])
    gate1 = C.din("gate1", [1, D])
    fing = C.din("fing", [1, D])
    out = C.dout("out", [NO, D])
    with C.st:
        yg = C.sb("yg", [128, 16, NO], BF16)
        wo = C.sb("wo", [128, 16, D], BF16)
        gbc = C.sb("gbc", [128, D], F32)
        fbc = C.sb("fbc", [128, D], F32)
        xt = C.sb("xt", [128, 2, D], F32)
        x2 = C.sb("x2", [128, 2, D], F32)
        junk = C.sb("junk", [128, D], BF16)
        ss = C.sb("ss", [128, 8], F32)
        rstd = C.sb("rstd", [128, 8], F32)
        psb = [C.ps("psb%d" % i, [128, 512], F32) for i in range(8)]
        S.dma(lambda e: e.dma_start(out=yg[:], in_=ygT), w=['yg'])
        for cg in range(4):
            S.dma(lambda e, cg=cg: e.dma_start(out=wo[:, :, cg * 512:(cg + 1) * 512],
                                               in_=w_out[:, cg * 512:(cg + 1) * 512].rearrange("(c p) n -> p c n", p=128)),
                  w=[('wo', cg)], q='pool')
        S.dma(lambda e: e.dma_start(out=gbc[:], in_=bass.AP(gate1.tensor, 0, [[0, 128], [1, D]])), w=['gbc'])
        S.dma(lambda e: e.dma_start(out=fbc[:], in_=bass.AP(fing.tensor, 0, [[0, 128], [1, D]])), w=['fbc'])
        S.pool(lambda e: e.memset(ss[:], 0.0), w=[('ss', i) for i in range(8)])
        for n in range(8):
            sl = n % 2
            S.dma(lambda e, n=n, sl=sl: e.dma_start(out=xt[:, sl, :], in_=x1[n * 128:(n + 1) * 128, :]), w=[('xt', sl)])
            for cg in range(4):
                pi = (n % 2) * 4 + cg
                for c in range(16):
                    S.pe(lambda e, c=c, cg=cg, pi=pi, n=n: e.matmul(
                        psb[pi][:, :], lhsT=yg[:, c, n * 128:(n + 1) * 128], rhs=wo[:, c, cg * 512:(cg + 1) * 512],
                        start=(c == 0), stop=(c == 15)), r=['yg', ('wo', cg)], w=[('ps', pi)])
                S.dve(lambda e, cg=cg, pi=pi, sl=sl: e.tensor_tensor(
                    out=x2[:, sl, cg * 512:(cg + 1) * 512], in0=psb[pi][:, :], in1=gbc[:, cg * 512:(cg + 1) * 512],
                    op=ALU.mult), r=[('ps', pi), 'gbc'], w=[('x2', sl)])
            S.pool(lambda e, sl=sl: e.tensor_tensor(out=x2[:, sl, :], in0=x2[:, sl, :], in1=xt[:, sl, :], op=ALU.add),
                   r=[('xt', sl), ('x2', sl)], w=[('x2', sl)])
            S.act(lambda e, n=n, sl=sl: e.activation(out=junk[:], in_=x2[:, sl, :], func=AF.Square, accum_out=ss[:, n:n + 1]),
                  r=[('x2', sl)], w=['junk', ('ss', n)])
            _rstd_ops(S, ss[:, n:n + 1], rstd[:, n:n + 1], ('ss', n), ('rstd', n))
            S.dve(lambda e, n=n, sl=sl: e.scalar_tensor_tensor(out=x2[:, sl, :], in0=x2[:, sl, :], scalar=rstd[:, n:n + 1],
                                                               in1=fbc[:], op0=ALU.mult, op1=ALU.mult),
                  r=[('x2', sl), ('rstd', n), 'fbc'], w=[('x2', sl)])
            S.dma(lambda e, n=n, sl=sl: e.dma_start(out=out[n * 128:(n + 1) * 128, :], in_=x2[:, sl, :]),
                  r=[('x2', sl)], w=['out'])
        S.emit(final_tokens=['out'])
    return nc


def _dft_tables():
    t = np.arange(2048, dtype=np.int64)
    f = np.arange(2048, dtype=np.int64)
    m = (np.outer(t, 2 * f + 1)) % 8192
    ang = 2.0 * np.pi * m.astype(np.float64) / 8192.0
    Cm = np.cos(ang)
    Sm = np.sin(ang)
    tabs = np.stack([Cm, Sm], 0)
    F = tabs.reshape(2, 16, 128, 8, 256).transpose(3, 2, 1, 0, 4)
    Ftab = np.ascontiguousarray(F.reshape(8, 128, 16 * 2 * 256)).astype(ml_dtypes.bfloat16)
    G = tabs.reshape(2, 4, 512, 16, 128)
    G = G.transpose(1, 3, 0, 4, 2)
    G = G.reshape(4, 32, 128, 512)
    G = G.reshape(4, 2, 16, 128, 512).transpose(0, 1, 3, 2, 4)
    Gtab = np.ascontiguousarray((G * (2.0 / 4096.0)).reshape(4, 2, 128, 16 * 512)).astype(ml_dtypes.bfloat16)
    return Ftab, Gtab


def _filter_consts():
    L = 2048
    f32 = np.float32
    t = np.linspace(0.0, 1.0, L, dtype=f32)[:, None]
    w = (2.0 * math.pi * np.arange(L, dtype=f32) / L).astype(f32)
    fr = np.linspace(1e-4, 15, 16, dtype=f32)
    ang = w[:, None] * fr[None]
    z = np.concatenate([t, np.cos(ang), -np.sin(ang)], axis=-1).astype(f32)
    deltas = np.linspace(math.log(1e-2) / 1.5, math.log(1e-2) / 0.3, 2048, dtype=f32)
    window = (np.exp(-t * np.abs(deltas)[None]) + 0.05).astype(f32)
    return np.ascontiguousarray(z.T), window


def run_l2(inp, hx1T, ncores=NCORES, npairs=2, dbg=False):
    Ftab, Gtab = _dft_tables()
    zfT, window = _filter_consts()
    ident = np.eye(128, dtype=np.float32)
    hw = inp["hy_w_in"][0]
    maps = []
    for i in range(ncores):
        ch = 256 * i + np.arange(256)
        cols = np.concatenate([k * 2048 + ch for k in range(4)])
        cw = np.zeros((128, 6, 4), np.float32)
        for k in range(3):
            for cch in range(2):
                cidx = k * 2048 + 256 * i + cch * 128 + np.arange(128)
                cw[:, k * 2 + cch, 0:3] = inp["hy_conv_w"][0][:, cidx].T
                cw[:, k * 2 + cch, 3] = inp["hy_conv_b"][0][cidx]
        dd = np.zeros((128, 2, 2), np.float32)
        for o in range(2):
            for cch in range(2):
                dd[:, o, cch] = inp["hy_bias_d"][0][o, 256 * i + cch * 128 + np.arange(128)]
        w3c = np.concatenate([(o * 2 + d) * 2048 + ch for o in range(2) for d in range(2)])
        fw3 = np.concatenate([inp["hy_w3"][0][:, w3c], inp["hy_b3"][0][w3c][None]], 0)
        fvec = np.stack([inp["hy_b1"][0], inp["hy_b2"][0], inp["hy_freq"][0], np.full(64, -math.pi, np.float32)], 1)
        wn = np.stack([window[:, ch], window[:, ch]], 1).copy()
        wn[0, 1, :] = 0.0
        maps.append({"hx1T": hx1T, "w_in": np.ascontiguousarray(hw[:, cols]), "cw": cw, "dd": dd, "zf": zfT,
                     "fw1": np.ascontiguousarray(inp["hy_w1"][0]), "fw2": np.ascontiguousarray(inp["hy_w2"][0]),
                     "fw3": np.ascontiguousarray(fw3.astype(np.float32)), "fvec": np.ascontiguousarray(fvec.astype(np.float32)),
                     "win": np.ascontiguousarray(wn.reshape(16, 128, 2, 256)), "Ftab": Ftab, "Gtab": Gtab, "identd": ident})
    res = run_bass_kernel_spmd(build_l2(npairs, dbg), maps, core_ids=list(range(ncores)))
    if dbg:
        return res.results
    yg = np.zeros((4, 128, 16, 2048), ml_dtypes.bfloat16)
    for i, r in enumerate(res.results):
        y = r["yg"]
        for cch in range(2):
            yg[:, :, i * 2 + cch, :] = y[:, cch].transpose(1, 0, 2)
    return yg


def run_l3(inp, yg, x1, mod):
    maps = []
    for i in range(NCORES):
        b, half = i // 2, i % 2
        gate1 = np.split(mod[1, b], 3)[2]
        maps.append({"ygT": np.ascontiguousarray(yg[b][:, :, half * 1024:(half + 1) * 1024]),
                     "w_out": np.ascontiguousarray(inp["hy_w_out"][0]),
                     "x1": np.ascontiguousarray(x1[b, half * 1024:(half + 1) * 1024]),
                     "gate1": np.ascontiguousarray(gate1[None]), "fing": np.ascontiguousarray(inp["final_g"][None])})
    res = run_bass_kernel_spmd(build_l3(), maps, core_ids=list(range(NCORES)))
    out = np.zeros((4, 2048, D), np.float32)
    for i, r in enumerate(res.results):
        b, half = i // 2, i % 2
        out[b, half * 1024:(half + 1) * 1024] = r["out"]
    return out


def kernel(**inputs):
    inp = {k: np.asarray(v) for k, v in inputs.items()}
    mod = run_l0(inp)
    x1, hx1T = run_l1(inp, mod)
    yg = run_l2(inp, hx1T)
    return run_l3(inp, yg, x1, mod)
```
